# Optimizing a Trainium2 kernel written in Bass

```python
import jax, jax.numpy as jnp
from jax import lax
import numpy as np

D_MODEL = 2048
BATCH = 8
SEQ = 2048
DEPTH = 1

HEAD_DIM = 128
HEADS_PER_GROUP = 8
DILATED_GROUPS = ((128, 1), (512, 4), (2048, 16))
N_GROUPS = len(DILATED_GROUPS)
ATTN_WIDTH = N_GROUPS * HEADS_PER_GROUP * HEAD_DIM
ATTN_OUT_WIDTH = HEADS_PER_GROUP * HEAD_DIM
CONV_WIDTH = D_MODEL
CONV_K = 3
IN_COLS = 3 * ATTN_WIDTH + 3 * CONV_WIDTH + 2 * D_MODEL
FFN_HIDDEN = -(-(8 * D_MODEL) // (3 * 256)) * 256
ALPHA = (2 * DEPTH) ** 0.25
BETA = (8 * DEPTH) ** -0.25
LN_EPS = 1e-5

kernel_name = "hybrid_dilated_attn_shortconv_gated_deepnorm"


def layer_norm(x, g, b):
    xf = x.astype(jnp.float32)
    mu = xf.mean(-1, keepdims=True)
    var = jnp.square(xf - mu).mean(-1, keepdims=True)
    y = (xf - mu) * lax.rsqrt(var + LN_EPS) * g.astype(jnp.float32) + b.astype(jnp.float32)
    return y.astype(x.dtype)


def dilated_band_attention(q, k, v, window, dilation):
    b, s, h, dh = q.shape
    band = window // dilation
    length = s // dilation
    n_blk = -(-length // band)
    pad = n_blk * band - length

    def to_blocks(t):
        t = t.reshape(b, length, dilation, h, dh).transpose(0, 2, 3, 1, 4)
        t = jnp.pad(t, ((0, 0), (0, 0), (0, 0), (0, pad), (0, 0)))
        return t.reshape(b, dilation, h, n_blk, band, dh)

    def with_prev_block(t):
        prev = jnp.pad(t, ((0, 0), (0, 0), (0, 0), (1, 0), (0, 0), (0, 0)))[:, :, :, :-1]
        return jnp.concatenate([prev, t], axis=4)

    qb = to_blocks(q)
    kb = with_prev_block(to_blocks(k))
    vb = with_prev_block(to_blocks(v))

    scores = jnp.einsum('brhnqd,brhnkd->brhnqk', qb, kb).astype(jnp.float32) * (dh ** -0.5)
    qi = jnp.arange(band)[:, None]
    kj = jnp.arange(2 * band)[None, :]
    dist = band + qi - kj
    in_band = (dist >= 0) & (dist <= band)
    first_blk = (jnp.arange(n_blk) == 0)[:, None, None]
    valid = in_band[None] & ~(first_blk & (kj[None] < band))
    scores = jnp.where(valid, scores, -jnp.inf)
    m = scores.max(-1, keepdims=True)
    p = jnp.exp(scores - m)
    denom = p.sum(-1)
    o = jnp.einsum('brhnqk,brhnkd->brhnqd', p, vb.astype(jnp.float32)) / denom[..., None]
    lse = m[..., 0] + jnp.log(denom)

    o = o.reshape(b, dilation, h, n_blk * band, dh)[:, :, :, :length]
    o = o.transpose(0, 3, 1, 2, 4).reshape(b, s, h, dh)
    lse = lse.reshape(b, dilation, h, n_blk * band)[:, :, :, :length]
    lse = lse.transpose(0, 3, 1, 2).reshape(b, s, h)
    return o, lse


def causal_short_conv(z, w):
    s = z.shape[1]
    zp = jnp.pad(z, ((0, 0), (CONV_K - 1, 0), (0, 0)))
    y = w[0] * z
    for tap in range(1, CONV_K):
        y = y + w[tap] * zp[:, CONV_K - 1 - tap: CONV_K - 1 - tap + s]
    return y


def setup_inputs(seed: int = 0) -> dict:
    key = jax.random.key(seed)
    ks = jax.random.split(key, 16)
    f32 = jnp.float32
    d = D_MODEL

    def nrm(k, shape, scale):
        return jax.random.normal(k, shape, f32) * scale

    x = jax.random.normal(ks[0], (BATCH, SEQ, d), f32)
    w_qk = nrm(ks[1], (DEPTH, d, 2 * ATTN_WIDTH), d ** -0.5)
    w_v = nrm(ks[2], (DEPTH, d, ATTN_WIDTH), BETA * d ** -0.5)
    w_conv_in = nrm(ks[3], (DEPTH, d, 3 * CONV_WIDTH), d ** -0.5)
    w_gates = nrm(ks[4], (DEPTH, d, 2 * d), d ** -0.5)
    w_in = jnp.concatenate([w_qk, w_v, w_conv_in, w_gates], axis=-1)
    conv_w = nrm(ks[5], (DEPTH, CONV_K, CONV_WIDTH), CONV_K ** -0.5)
    w_attn_o = nrm(ks[6], (DEPTH, ATTN_OUT_WIDTH, d), BETA * ATTN_OUT_WIDTH ** -0.5)
    w_conv_o = nrm(ks[7], (DEPTH, CONV_WIDTH, d), BETA * CONV_WIDTH ** -0.5)
    w_out = nrm(ks[8], (DEPTH, d, d), BETA * d ** -0.5)
    ln1_g = 1.0 + nrm(ks[9], (DEPTH, d), 0.02)
    ln1_b = nrm(ks[10], (DEPTH, d), 0.02)
    w_ffn_gate = nrm(ks[11], (DEPTH, d, FFN_HIDDEN), d ** -0.5)
    w_ffn_up = nrm(ks[12], (DEPTH, d, FFN_HIDDEN), BETA * d ** -0.5)
    w_ffn_down = nrm(ks[13], (DEPTH, FFN_HIDDEN, d), BETA * FFN_HIDDEN ** -0.5)
    ln2_g = 1.0 + nrm(ks[14], (DEPTH, d), 0.02)
    ln2_b = nrm(ks[15], (DEPTH, d), 0.02)
    return {"x": x, "w_in": w_in, "conv_w": conv_w, "w_attn_o": w_attn_o,
            "w_conv_o": w_conv_o, "w_out": w_out, "ln1_g": ln1_g, "ln1_b": ln1_b,
            "w_ffn_gate": w_ffn_gate, "w_ffn_up": w_ffn_up, "w_ffn_down": w_ffn_down,
            "ln2_g": ln2_g, "ln2_b": ln2_b}


def reference(x, w_in, conv_w, w_attn_o, w_conv_o, w_out, ln1_g, ln1_b,
              w_ffn_gate, w_ffn_up, w_ffn_down, ln2_g, ln2_b):
    b, s, _ = x.shape
    cols = (ATTN_WIDTH,) * 3 + (CONV_WIDTH,) * 3 + (D_MODEL,) * 2
    split_points = [int(c) for c in np.cumsum(cols)[:-1]]
    for layer in range(DEPTH):
        proj = x @ w_in[layer]
        q, k, v, u, c_gate, b_gate, g_attn, g_conv = jnp.split(proj, split_points, axis=-1)

        q = q.reshape(b, s, N_GROUPS, HEADS_PER_GROUP, HEAD_DIM)
        k = k.reshape(b, s, N_GROUPS, HEADS_PER_GROUP, HEAD_DIM)
        v = v.reshape(b, s, N_GROUPS, HEADS_PER_GROUP, HEAD_DIM)
        outs, lses = [], []
        for g, (window, dilation) in enumerate(DILATED_GROUPS):
            o_g, lse_g = dilated_band_attention(q[:, :, g], k[:, :, g], v[:, :, g], window, dilation)
            outs.append(o_g)
            lses.append(lse_g)
        mix_w = jax.nn.softmax(jnp.stack(lses, axis=0), axis=0)
        attn = jnp.sum(mix_w[..., None] * jnp.stack(outs, axis=0), axis=0)
        attn = attn.reshape(b, s, ATTN_OUT_WIDTH).astype(x.dtype) @ w_attn_o[layer]

        conv = (b_gate * causal_short_conv(c_gate * u, conv_w[layer])) @ w_conv_o[layer]

        merged = (jax.nn.sigmoid(g_attn) * attn + jax.nn.sigmoid(g_conv) * conv) @ w_out[layer]
        x = layer_norm(ALPHA * x + merged, ln1_g[layer], ln1_b[layer])

        hidden = jax.nn.silu(x @ w_ffn_gate[layer]) * (x @ w_ffn_up[layer])
        x = layer_norm(ALPHA * x + hidden @ w_ffn_down[layer], ln2_g[layer], ln2_b[layer])
    return x
```

```python
import numpy as np
from contextlib import ExitStack
import concourse.bass as bass
import concourse.mybir as mybir
from concourse.bass_utils import run_bass_kernel_spmd

F32 = mybir.dt.float32
BF16 = mybir.dt.bfloat16
AF = mybir.ActivationFunctionType
ALU = mybir.AluOpType

D = 2048
S = 2048
NCORES = 8
HD = 128
NHG = 8
GROUPS = ((128, 1), (512, 4), (2048, 16))
AW = 3 * NHG * HD
FF = 5632
NFC = FF // 128
IN_COLS = 3 * AW + 3 * D + 2 * D
ALPHA = 2.0 ** 0.25
EPS = 1e-5
SCALE = HD ** -0.5
TT = 512
NT = S // TT

C_Q, C_K, C_V = 0, AW, 2 * AW
C_U, C_C, C_B = 3 * AW, 3 * AW + D, 3 * AW + 2 * D
C_GA, C_GC = 3 * AW + 3 * D, 3 * AW + 4 * D


class Buf:
    __slots__ = ("name", "writers", "readers")

    def __init__(self, name):
        self.name = name
        self.writers = {}
        self.readers = {}

    def alias_from(self, others):
        for o in others:
            for k, v in o.writers.items():
                self.readers[("w", k)] = v
            for k, v in o.readers.items():
                self.readers[("r", k)] = v


class Op:
    __slots__ = ("eng", "fn", "deps", "signal", "idx", "dma_sem", "dma_val", "key")

    def __init__(self, eng, fn):
        self.eng = eng
        self.fn = fn
        self.deps = []
        self.signal = False
        self.idx = None
        self.dma_sem = None
        self.dma_val = None
        self.key = eng


class DmaSem:
    __slots__ = ("sem", "count", "uid")
    _n = 0

    def __init__(self, sem):
        self.sem = sem
        self.count = 0
        DmaSem._n += 1
        self.uid = DmaSem._n


class Sched:
    ENGS = ("pe", "act", "dve", "pool", "sp")

    def __init__(self):
        self.streams = {e: [] for e in self.ENGS}

    def _link(self, op, reads, writes):
        deps = {}
        for b in reads:
            for w in b.writers.values():
                deps[id(w)] = (w, True)
        for b in writes:
            for r in b.readers.values():
                if id(r) not in deps:
                    deps[id(r)] = (r, False)
            for w in b.writers.values():
                if id(w) not in deps:
                    deps[id(w)] = (w, False)
        for d, raw in deps.values():
            if d is op:
                continue
            if d.dma_sem is None and op.dma_sem is None and d.eng == op.eng:
                if op.eng == "pe" or not raw:
                    continue
            if d.dma_sem is None:
                d.signal = True
            op.deps.append(d)
        for b in reads:
            b.readers[op.key] = op
        for b in writes:
            b.writers[op.key] = op

    def op(self, eng, fn, reads=(), writes=()):
        o = Op(eng, fn)
        self._link(o, reads, writes)
        self.streams[eng].append(o)
        return o

    def dma(self, queue, dsem, out, in_, reads=(), writes=()):
        o = Op(queue, lambda e: e.dma_start(out=out, in_=in_))
        dsem.count += 1
        o.dma_sem = dsem
        o.dma_val = 16 * dsem.count
        o.key = ("dma", dsem.uid)
        self._link(o, reads, writes)
        self.streams[queue].append(o)
        return o

    def emit(self, nc, block, esems, final_waits):
        for e in self.ENGS:
            n = 0
            for o in self.streams[e]:
                if o.dma_sem is None and o.signal:
                    n += 1
                    o.idx = n

        def run(ename, eng):
            waited = {}
            for o in self.streams[ename]:
                for d in o.deps:
                    if d.dma_sem is not None:
                        sem, val, k = d.dma_sem.sem, d.dma_val, ("d", d.dma_sem.uid)
                    else:
                        sem, val, k = esems[d.eng], d.idx, ("e", d.eng)
                    if waited.get(k, 0) >= val:
                        continue
                    waited[k] = val
                    eng.wait_ge(sem, val)
                ins = o.fn(eng)
                if o.dma_sem is not None:
                    ins.then_inc(o.dma_sem.sem, 16)
                elif o.signal:
                    ins.then_inc(esems[ename], 1)
            if ename == "sp":
                for ds in final_waits:
                    eng.wait_ge(ds.sem, 16 * ds.count)

        @block.tensor
        def _(t):
            run("pe", t)

        @block.scalar
        def _(s):
            run("act", s)

        @block.vector
        def _(v):
            run("dve", v)

        @block.gpsimd
        def _(g):
            run("pool", g)

        @block.sync
        def _(sp):
            run("sp", sp)


def build_program(debug=False, n_heads=NHG, n_tiles=NT):
    nc = bass.Bass("TRN2", target_bir_lowering=False)
    x_d = nc.dram_tensor("x", [S, D], F32, kind="ExternalInput").ap()
    w_in = nc.dram_tensor("w_in", [D, IN_COLS], F32, kind="ExternalInput").ap()
    conv_w = nc.dram_tensor("conv_w", [128, 16 * 3], F32, kind="ExternalInput").ap()
    w_ao = nc.dram_tensor("w_attn_o", [NHG * HD, D], F32, kind="ExternalInput").ap()
    w_co = nc.dram_tensor("w_conv_o", [D, D], F32, kind="ExternalInput").ap()
    w_o = nc.dram_tensor("w_out", [D, D], F32, kind="ExternalInput").ap()
    w_g = nc.dram_tensor("w_ffn_gate", [D, FF], F32, kind="ExternalInput").ap()
    w_u = nc.dram_tensor("w_ffn_up", [D, FF], F32, kind="ExternalInput").ap()
    w_d = nc.dram_tensor("w_ffn_down", [FF, D], F32, kind="ExternalInput").ap()
    ln_d = nc.dram_tensor("ln_params", [4, 128, D], F32, kind="ExternalInput").ap()
    lnT_d = nc.dram_tensor("ln_t", [128, 32], F32, kind="ExternalInput").ap()
    y_d = nc.dram_tensor("y", [S, D], F32, kind="ExternalOutput").ap()
    NB_IN = IN_COLS - C_U
    wb_in = nc.dram_tensor("wb_in", [D, NB_IN], BF16, kind="Internal").ap()
    wb_ao = nc.dram_tensor("wb_ao", [NHG * HD, D], BF16, kind="Internal").ap()
    wb_co = nc.dram_tensor("wb_co", [D, D], BF16, kind="Internal").ap()
    wb_o = nc.dram_tensor("wb_o", [D, D], BF16, kind="Internal").ap()
    wb_g = nc.dram_tensor("wb_g", [D, FF], BF16, kind="Internal").ap()
    wb_u = nc.dram_tensor("wb_u", [D, FF], BF16, kind="Internal").ap()
    wb_d = nc.dram_tensor("wb_d", [FF, D], BF16, kind="Internal").ap()
    if debug:
        dbg_attn = nc.dram_tensor("dbg_attn", [128, NHG, S], BF16, kind="ExternalOutput").ap()
        dbg_x1 = nc.dram_tensor("dbg_x1", [S, D], F32, kind="ExternalOutput").ap()

    es = ExitStack()
    sch = Sched()

    def sb(name, shape, dt):
        return es.enter_context(nc.sbuf_tensor(name, shape, dt))

    def mksem(name):
        return es.enter_context(nc.semaphore(name))

    def dsem(name):
        return DmaSem(mksem(name))

    ident = sb("ident", [128, 128], BF16)
    ones = sb("ones", [128, 128], BF16)
    maskLU = sb("maskLU", [128, 512], BF16)
    maskU4 = sb("maskU4", [128, 512], BF16)
    cw = sb("cw", [128, 48], F32)
    halo = sb("halo", [128, 32], F32)
    attnT = sb("attnT", [128, NHG, S], BF16)
    arena = sb("arena", [128, 16 * S], BF16)
    NSLAB = 4
    slabs = sb("slabs", [128, NSLAB, 4096], BF16)
    xin = sb("xin", [128, 2, D], BF16)
    x1buf = sb("x1buf", [128, 4, D], F32)
    lnp = sb("lnp", [128, 2, D], F32)
    NTMP = 10
    tmps = sb("tmps", [128, NTMP, 516], F32)
    small = sb("small", [128, 4, 32], F32)
    lnT = sb("lnT", [128, 32], F32)

    xT = arena[:, :].rearrange("p (c t) -> p c t", c=16)
    x1bf = x1buf[:, :, :].rearrange("p a d -> p (a d)").bitcast(BF16)
    qkT = x1bf[:, 0:8192].rearrange("p (s w t) -> p s w t", s=2, w=2)
    vsb = x1bf[:, 8192:12288].rearrange("p (s b d) -> p s b d", s=2, b=16)
    EPT = x1bf[:, 12288:14336].rearrange("p (s w c) -> p s w c", s=2, w=2)
    numden = lnp
    xTt = arena[:, 0:8192].rearrange("p (c t) -> p c t", c=16)
    cinT = arena[:, 8192:16384].rearrange("p (c t) -> p c t", c=16)
    mrgT = arena[:, 16384:24576].rearrange("p (c t) -> p c t", c=16)
    hT = arena[:, 0:NFC * TT].rearrange("p (c t) -> p c t", c=NFC)
    x1T = arena[:, 24576:32768].rearrange("p (c t) -> p c t", c=16)

    banks = [es.enter_context(nc.psum_tensor(f"pb{i}", [128, 512], F32)) for i in range(8)]
    bank_bufs = [Buf(f"bank{i}") for i in range(8)]
    bank_rr = [0]

    pool_rr = {}

    def next_bank(pool=None):
        if pool is None:
            i = bank_rr[0] % 8
            bank_rr[0] += 1
        else:
            k = pool_rr.get(pool, 0)
            pool_rr[pool] = k + 1
            i = pool[k % len(pool)]
        return banks[i], bank_bufs[i]

    P_ACC = (0, 1, 2, 3)
    P_S = (4, 5)
    P_PROJ = (6, 7)

    esems = {e: mksem(f"sem_{e}") for e in Sched.ENGS}

    B_const = Buf("const")
    B_xT = [Buf(f"xT{i}") for i in range(4)]
    B_attnT = [Buf(f"attnT{h}") for h in range(NHG)]
    B_xin = [Buf("xin0"), Buf("xin1")]
    S_xin = [dsem("s_xin0"), dsem("s_xin1")]
    B_slab = [Buf(f"slab{i}") for i in range(NSLAB)]
    S_slab = [dsem(f"s_slab{i}") for i in range(NSLAB)]
    slab_rr = [0]
    B_qk = [[Buf(f"qT{s}"), Buf(f"kT{s}")] for s in range(2)]
    B_v = [Buf("v0"), Buf("v1")]
    B_E = [Buf("E0"), Buf("E1")]
    B_PT = [Buf("PT0"), Buf("PT1")]
    B_num = Buf("num")
    B_den = Buf("den")
    B_small = [Buf(f"small{i}") for i in range(4)]

    def load_slab(src, a, b):
        i = slab_rr[0] % NSLAB
        slab_rr[0] += 1
        assert a * b <= 4096
        view = slabs[:, i, 0:a * b].rearrange("p (a b) -> p a b", a=a)
        sch.dma("pool", S_slab[i], view, src, writes=[B_slab[i]])
        return view, B_slab[i]

    def load_slab_bf(src, a, b, B_src):
        i = slab_rr[0] % NSLAB
        slab_rr[0] += 1
        assert a * b <= 4096
        view = slabs[:, i, 0:a * b].rearrange("p (a b) -> p a b", a=a)
        sch.dma("sp", S_slab[i], view, src, reads=[B_src], writes=[B_slab[i]])
        return view, B_slab[i]

    G_names = ("g1", "g2", "g3", "g4", "g5")
    B_grp = {n: Buf("cv_" + n) for n in G_names}
    S_grp = {n: dsem("s_cv_" + n) for n in G_names}
    conv_list = []

    def add_conv(grp, dst, src_, rows, ncols):
        for r0 in range(0, rows, 128):
            conv_list.append((128 * ncols * 4, grp, dst[r0:r0 + 128, :], src_[r0:r0 + 128, :]))

    add_conv("g1", wb_in[:, 0:3 * D], w_in[:, C_U:C_U + 3 * D], D, 3 * D)
    add_conv("g2", wb_in[:, 3 * D:5 * D], w_in[:, C_GA:C_GA + 2 * D], D, 2 * D)
    add_conv("g2", wb_ao, w_ao, NHG * HD, D)
    add_conv("g2", wb_co, w_co, D, D)
    add_conv("g3", wb_o, w_o, D, D)
    conv_total = sum(c[0] for c in conv_list)
    conv_state = {"i": 0, "bytes": 0}

    def issue_conversions(frac):
        while conv_state["i"] < len(conv_list) and conv_state["bytes"] < frac * conv_total:
            nb, grp, dst, src_ = conv_list[conv_state["i"]]
            sch.dma("pool", S_grp[grp], dst, src_, writes=[B_grp[grp]])
            conv_state["i"] += 1
            conv_state["bytes"] += nb

    def colslab(w, c0, nc_=128):
        return w[:, c0:c0 + nc_].rearrange("(kc p) c -> p kc c", p=128)

    evac_rr = [0]

    def evac_engine():
        evac_rr[0] += 1
        return "act" if evac_rr[0] % 2 else "dve"

    def copy_op(eng, out, in_, reads, writes):
        if eng == "act":
            sch.op("act", lambda e: e.activation(out=out, in_=in_, func=AF.Copy), reads, writes)
        else:
            sch.op(eng, lambda e: e.tensor_copy(out=out, in_=in_), reads, writes)


    def mm(out, lhsT, rhs, start, stop, reads, writes):
        sch.op("pe", lambda e: e.matmul(out, lhsT, rhs, start=start, stop=stop), reads, writes)

    def tp(out, in_, reads, writes):
        sch.op("pe", lambda e: e.transpose(out=out, in_=in_, identity=ident[:, :]), reads, writes)

    def act(out, in_, func, reads, writes, **kw):
        sch.op("act", lambda e: e.activation(out=out, in_=in_, func=func, **kw), reads, writes)

    def vtt(out, in0, in1, op, reads, writes, eng="dve"):
        sch.op(eng, lambda e: e.tensor_tensor(out=out, in0=in0, in1=in1, op=op), reads, writes)

    def vts(out, in0, s1, s2, op0, op1, reads, writes):
        if s2 is None:
            sch.op("dve", lambda e: e.tensor_scalar(out=out, in0=in0, scalar1=s1, scalar2=None, op0=op0), reads, writes)
        else:
            sch.op("dve", lambda e: e.tensor_scalar(out=out, in0=in0, scalar1=s1, scalar2=s2, op0=op0, op1=op1), reads, writes)

    def vstt(out, in0, scalar, in1, op0, op1, reads, writes):
        sch.op("dve", lambda e: e.scalar_tensor_tensor(out=out, in0=in0, scalar=scalar, in1=in1, op0=op0, op1=op1),
               reads, writes)

    def vcp(out, in_, reads, writes, eng="dve"):
        sch.op(eng, lambda e: e.tensor_copy(out=out, in_=in_), reads, writes)

    sch.op("pool", lambda e: e.memset(ident[:, :], 1.0), writes=[B_const])
    sch.op("pool", lambda e: e.affine_select(out=ident[:, :], in_=ident[:, :], pattern=[[-1, 128]],
                                             compare_op=ALU.is_equal, fill=0.0, base=0,
                                             channel_multiplier=1), reads=[B_const], writes=[B_const])
    sch.op("pool", lambda e: e.memset(ones[:, :], 1.0), writes=[B_const])
    sch.op("pool", lambda e: e.memset(maskLU[:, :], 1.0), writes=[B_const])
    sch.op("pool", lambda e: e.memset(maskU4[:, :], 1.0), writes=[B_const])
    for s in range(4):
        cs = slice(s * 128, (s + 1) * 128)
        if s % 2 == 0:
            sch.op("pool", lambda e, cs=cs: e.affine_select(
                out=maskLU[:, cs], in_=maskLU[:, cs], pattern=[[-1, 128]], compare_op=ALU.is_ge,
                fill=0.0, base=0, channel_multiplier=1), reads=[B_const], writes=[B_const])
        else:
            sch.op("pool", lambda e, cs=cs: e.affine_select(
                out=maskLU[:, cs], in_=maskLU[:, cs], pattern=[[1, 128]], compare_op=ALU.is_ge,
                fill=0.0, base=0, channel_multiplier=-1), reads=[B_const], writes=[B_const])
        sch.op("pool", lambda e, cs=cs: e.affine_select(
            out=maskU4[:, cs], in_=maskU4[:, cs], pattern=[[1, 128]], compare_op=ALU.is_ge,
            fill=0.0, base=0, channel_multiplier=-1), reads=[B_const], writes=[B_const])
    S_cw = dsem("s_cw")
    sch.dma("sp", S_cw, cw[:, :], conv_w, writes=[B_const])
    sch.dma("sp", S_cw, lnT[:, :], lnT_d, writes=[B_const])

    B_halo = Buf("halo")
    sch.op("pool", lambda e: e.memset(halo[:, :], 0.0), writes=[B_halo])

    def load_xT(dst, B_dst, tok0, nblk, after_tb=None):
        for tb in range(nblk):
            Bd = B_dst[tb // 4] if isinstance(B_dst, list) else B_dst
            sl = tb % 2
            sch.dma("pool", S_xin[sl], xin[:, sl, :], x_d[tok0 + tb * 128: tok0 + (tb + 1) * 128, :],
                    writes=[B_xin[sl]])
            for half in range(2):
                bk, bb = next_bank()
                bkb = bk[:, :].bitcast(BF16)
                for c in range(8):
                    kc = half * 8 + c
                    tp(bkb[:, c * 128:(c + 1) * 128], xin[:, sl, kc * 128:(kc + 1) * 128],
                       [B_xin[sl], B_const], [bb])
                copy_op(evac_engine(), dst[:, half * 8:(half + 1) * 8, tb * 128:(tb + 1) * 128],
                        bkb.rearrange("p (c t) -> p c t", c=8), [bb], [Bd])
            if after_tb is not None:
                after_tb(tb)

    def tok_ap(T2d, dil, r, n):
        if dil == 1:
            return T2d[:, n * 128:(n + 1) * 128]
        return T2d.rearrange("p (l r) -> p r l", r=dil)[:, r, n * 128:(n + 1) * 128]

    xT_tiles = [[xT[:, kc, tt * 512:(tt + 1) * 512] for kc in range(16)] for tt in range(4)]

    def make_head(h, g, dil, sl):
        n_blk = 16 // dil
        col = g * NHG * HD + h * HD
        qT = qkT[:, sl, 0, :]
        kT = qkT[:, sl, 1, :]
        BqT, BkT = B_qk[sl]
        W = {}

        def load_w():
            W["q"] = load_slab(colslab(w_in, C_Q + col), 16, 128)
            W["k"] = load_slab(colslab(w_in, C_K + col), 16, 128)
            W["v"] = load_slab(colslab(w_in, C_V + col), 16, 128)
            conv_state["heads"] = conv_state.get("heads", 0) + 1
            if conv_state["heads"] >= 2:
                issue_conversions((conv_state["heads"] - 1) / (2.25 * n_heads))

        def proj_group(which, dst, B_dst, tt):
            if not W:
                load_w()
            wv, B_w = W[which]
            bk, bb = next_bank(P_PROJ)
            for kc in range(16):
                mm(bk[:, :], wv[:, kc, :], xT_tiles[tt][kc], kc == 0, kc == 15, [B_w, B_xT[tt]], [bb])
            copy_op(evac_engine(), dst[:, tt * 512:(tt + 1) * 512], bk[:, :], [bb], [B_dst])

        def v_group(b4):
            if not W:
                load_w()
            wvv, Bwv = W["v"]
            bk, bb = next_bank(P_PROJ)
            for j in range(4):
                b = b4 * 4 + j
                r, n = b // n_blk, b % n_blk
                for kc in range(16):
                    mm(bk[:, j * 128:(j + 1) * 128], tok_ap(xT[:, kc, :], dil, r, n), wvv[:, kc, :],
                       kc == 0, kc == 15, ([B_xT[b4]] if dil == 1 else B_xT) + [Bwv], [bb])
            copy_op(evac_engine(), vsb[:, sl, b4 * 4:(b4 + 1) * 4, :],
                    bk[:, :].rearrange("p (b d) -> p b d", b=4), [bb], [B_v[sl]])

        psteps = []
        for tt in range(4):
            psteps.append(lambda tt=tt: proj_group("q", qT, BqT, tt))
        for tt in range(4):
            psteps.append(lambda tt=tt: proj_group("k", kT, BkT, tt))
        for b4 in range(4):
            psteps.append(lambda b4=b4: v_group(b4))

        units = []
        if dil == 16:
            for u in range(4):
                units.append([(b, b) for b in range(4 * u, 4 * u + 4)])
            mask = maskU4
        else:
            for u in range(8):
                ent = []
                for b in (2 * u, 2 * u + 1):
                    n = b % n_blk
                    ent.append((b - 1 if n > 0 else None, b))
                    ent.append((b, b))
                units.append(ent)
            mask = maskLU
        nu = len(units)
        ust = {}

        def issue_S(u):
            bk, bb = next_bank(P_S)
            ent = units[u]
            c0 = 128 if ent[0][0] is None else 0
            for s_i, (kb, qb) in enumerate(ent):
                if kb is None:
                    continue
                mm(bk[:, s_i * 128:(s_i + 1) * 128], tok_ap(kT, dil, kb // n_blk, kb % n_blk),
                   tok_ap(qT, dil, qb // n_blk, qb % n_blk), True, True, [BqT, BkT], [bb])
            e_ = u % 2
            Ev = EPT[:, e_, 0, :]
            Pv = EPT[:, e_, 1, :]
            act(Ev[:, c0:512], bk[:, c0:512], AF.Exp, [bb], [B_E[e_]], scale=SCALE)
            vtt(Pv[:, c0:512], Ev[:, c0:512], mask[:, c0:512], ALU.mult, [B_E[e_], B_const], [B_PT[e_]])
            ust[u] = (Pv, B_PT[e_])

        acc = {}

        def issue_PV(u):
            Pv, BP = ust[u]
            ent = units[u]
            qbs = sorted(set(q for _, q in ent))
            for qb in qbs:
                if qb % 4 == 0:
                    acc["o"] = next_bank(P_ACC)
                    acc["d"] = next_bank(P_ACC)
                sl_e = [(s_i, kb) for s_i, (kb, q) in enumerate(ent) if q == qb and kb is not None]
                qs = qb % 4
                for which in ("o", "d"):
                    tbk, tbb = acc[which]
                    for i_, (s_i, kb) in enumerate(sl_e):
                        lhs = vsb[:, sl, kb, :] if which == "o" else ones[:, :]
                        rd = [B_v[sl], BP] if which == "o" else [B_const, BP]
                        mm(tbk[:, qs * 128:(qs + 1) * 128], lhs, Pv[:, s_i * 128:(s_i + 1) * 128],
                           i_ == 0, i_ == len(sl_e) - 1, rd, [tbb])
                if qb % 4 == 3:
                    j = qb // 4
                    for which, Bacc in (("o", B_num), ("d", B_den)):
                        tbk, tbb = acc[which]
                        A = numden[:, 0 if which == "o" else 1, :]
                        if dil == 1:
                            dst = A[:, j * 512:(j + 1) * 512]
                            src_ = tbk[:, :]
                        elif dil == 4:
                            dst = A.rearrange("p (l r) -> p r l", r=4)[:, j, :]
                            src_ = tbk[:, :]
                        else:
                            dst = A.rearrange("p (i r) -> p r i", r=16)[:, 4 * j:4 * j + 4, :]
                            src_ = tbk[:, :].rearrange("p (r i) -> p r i", r=4)
                        if g == 0:
                            vcp(dst, src_, [tbb], [Bacc])
                        else:
                            vtt(dst, src_, dst, ALU.add, [tbb, Bacc], [Bacc])

        LOOK = 2
        asteps = []
        for u in range(min(LOOK, nu)):
            asteps.append(lambda u=u: issue_S(u))
        for u in range(nu):
            def st(u=u):
                issue_PV(u)
                if u + LOOK < nu:
                    issue_S(u + LOOK)
            asteps.append(st)
        fins = []
        if g == 2:
            for qd in range(1):
                def fin(qd=qd):
                    cs = slice(0, 2048)
                    sch.op("dve", (lambda o_: (lambda e: e.reciprocal(out=o_, in_=o_)))(numden[:, 1, cs]),
                           [B_den], [B_den])
                    vtt(attnT[:, h, cs], numden[:, 0, cs], numden[:, 1, cs], ALU.mult, [B_num, B_den], [B_attnT[h]])
                fins.append(fin)
        return psteps, asteps, fins

    heads = []
    for h in range(n_heads):
        for g, (window, dil) in enumerate(GROUPS):
            heads.append(make_head(h, g, dil, len(heads) % 2))
    def after_tb(tb):
        if tb % 4 == 3:
            tt = tb // 4
            for idx in (tt, 4 + tt, 8 + tt):
                heads[0][0][idx]()
    load_xT(xT, B_xT, 0, 16, after_tb)
    prev_fins = []
    for i, (_, A_, F_) in enumerate(heads):
        P_ = heads[i + 1][0] if i + 1 < len(heads) else []
        k = 0
        for si, a_ in enumerate(A_):
            a_()
            while prev_fins:
                prev_fins.pop(0)()
            target = ((si + 1) * len(P_)) // len(A_)
            while k < target:
                P_[k]()
                k += 1
        assert not prev_fins
        prev_fins = list(F_)
    for f_ in prev_fins:
        f_()

    out_sems = []
    if debug:
        S_dbg = dsem("s_dbg")
        sch.dma("pool", S_dbg, dbg_attn, attnT[:, :, :], reads=B_attnT)
        out_sems.append(S_dbg)

    issue_conversions(2.0)
    B_lnp = Buf("lnp")
    B_lnp.alias_from([B_num, B_den])
    S_lnp = dsem("s_lnp")
    B_x1 = [Buf(f"x1_{i}") for i in range(4)]
    for bx in B_x1:
        bx.alias_from(B_qk[0] + B_qk[1] + B_v + B_E + B_PT)
    S_x1 = [dsem(f"s_x1_{i}") for i in range(4)]
    S_out = [dsem(f"s_out_{i}") for i in range(4)]
    out_sems += S_out
    B_xTt = Buf("xTt")
    B_cin = Buf("cinT")
    B_mrg = Buf("mrgT")
    B_hT = Buf("hT")
    B_x1T = [Buf(f"x1T{i}") for i in range(16)]
    for bb_ in [B_xTt, B_cin, B_mrg] + B_x1T:
        bb_.alias_from(B_xT)
    B_tmp = [Buf(f"tmp{i}") for i in range(NTMP)]
    tmp_rr = [0]

    def tmp():
        i = tmp_rr[0] % NTMP
        tmp_rr[0] += 1
        return tmps[:, i, :], B_tmp[i]

    def cws(j, tap):
        return cw[:, j * 3 + tap: j * 3 + tap + 1]

    def ln_stats(tb):
        yv = x1buf[:, tb, :]
        Bx, Bs = B_x1[tb], B_small[tb]
        sm = small[:, tb, :]
        for c in range(4):
            sch.op("dve", (lambda o_, i_: (lambda e: e.bn_stats(out=o_, in_=i_)))(
                sm[:, c * 6:(c + 1) * 6], yv[:, c * 512:(c + 1) * 512]), [Bx], [Bs])
        sch.op("dve", lambda e: e.bn_aggr(out=sm[:, 24:26], in_=sm[:, 0:24]), [Bs], [Bs])
        act(sm[:, 28:29], sm[:, 25:26], AF.Sqrt, [Bs], [Bs], bias=EPS, scale=1.0)
        sch.op("dve", lambda e: e.reciprocal(out=sm[:, 26:27], in_=sm[:, 28:29]), [Bs], [Bs])
        vstt(sm[:, 27:28], sm[:, 24:25], -1.0, sm[:, 26:27], ALU.mult, ALU.mult, [Bs], [Bs])

    def ln_finish(tb, bias_eng="dve"):
        yv = x1buf[:, tb, :]
        Bx, Bs = B_x1[tb], B_small[tb]
        sm = small[:, tb, :]
        act(yv, yv, AF.Identity, [Bx, Bs], [Bx], bias=sm[:, 27:28], scale=sm[:, 26:27])
        vtt(yv, yv, lnp[:, 0, :], ALU.mult, [Bx, B_lnp], [Bx])
        vtt(yv, yv, lnp[:, 1, :], ALU.add, [Bx, B_lnp], [Bx], eng=bias_eng)

    pending_ln2 = []

    for t in range(n_tiles):
        t0 = t * TT
        load_xT(xTt, B_xTt, t0, 4)
        xTt_kc = [xTt[:, kc, :] for kc in range(16)]
        for jp in range(8):
            wu2, Bwu = load_slab_bf(colslab(wb_in, jp * 256, 256), 16, 256, B_grp['g1'])
            wc2, Bwc = load_slab_bf(colslab(wb_in, D + jp * 256, 256), 16, 256, B_grp['g1'])
            wb2, Bwb = load_slab_bf(colslab(wb_in, 2 * D + jp * 256, 256), 16, 256, B_grp['g1'])
            pbs = {}
            for nm, wv_, Bw_ in (("u", wu2, Bwu), ("c", wc2, Bwc), ("b", wb2, Bwb)):
                for j2 in range(2):
                    bk, bb = next_bank()
                    for kc in range(16):
                        mm(bk[:, :], wv_[:, kc, j2 * 128:(j2 + 1) * 128], xTt_kc[kc], kc == 0, kc == 15,
                           [Bw_, B_xTt], [bb])
                    pbs[(nm, j2)] = (bk, bb)
                    if nm == "u":
                        us, Bus = tmp()
                        act(us[:, 0:512], bk[:, :], AF.Copy, [bb], [Bus])
                        pbs[("us", j2)] = (us, Bus)
                    elif nm == "c":
                        j = 2 * jp + j2
                        us, Bus = pbs[("us", j2)]
                        zs, Bzs = tmp()
                        vcp(zs[:, 0:2], halo[:, 2 * j:2 * j + 2], [B_halo], [Bzs])
                        vtt(zs[:, 2:514], bk[:, :], us[:, 0:512], ALU.mult, [bb, Bus], [Bzs])
                        vcp(halo[:, 2 * j:2 * j + 2], zs[:, 512:514], [Bzs], [B_halo])
                        ya, Bya = us, Bus
                        vts(ya[:, 0:512], zs[:, 2:514], cws(j, 0), None, ALU.mult, None, [Bzs, B_const], [Bya])
                        vstt(ya[:, 0:512], zs[:, 1:513], cws(j, 1), ya[:, 0:512], ALU.mult, ALU.add,
                             [Bzs, Bya, B_const], [Bya])
                        vstt(ya[:, 0:512], zs[:, 0:512], cws(j, 2), ya[:, 0:512], ALU.mult, ALU.add,
                             [Bzs, Bya, B_const], [Bya])
                    else:
                        j = 2 * jp + j2
                        ya, Bya = pbs[("us", j2)]
                        vtt(cinT[:, j, :], bk[:, :], ya[:, 0:512], ALU.mult, [bb, Bya], [B_cin])
            if pending_ln2:
                pending_ln2.pop(0)()
        attn_kc = [attnT[:, kc, t0:t0 + TT] for kc in range(NHG)]
        cin_kc = [cinT[:, kc, :] for kc in range(16)]
        for jp in range(8):
            wga, Bwga = load_slab_bf(colslab(wb_in, 3 * D + jp * 256, 256), 16, 256, B_grp['g2'])
            wao, Bwao = load_slab_bf(colslab(wb_ao, jp * 256, 256), 8, 256, B_grp['g2'])
            wgc, Bwgc = load_slab_bf(colslab(wb_in, 4 * D + jp * 256, 256), 16, 256, B_grp['g2'])
            wco, Bwco = load_slab_bf(colslab(wb_co, jp * 256, 256), 16, 256, B_grp['g2'])
            st_ = {}
            for nm, wv_, Bw_, rhs_l, rB in (("ga", wga, Bwga, xTt_kc, [B_xTt]), ("A", wao, Bwao, attn_kc, B_attnT),
                                            ("gc", wgc, Bwgc, xTt_kc, [B_xTt]), ("C", wco, Bwco, cin_kc, [B_cin])):
                nk = len(rhs_l)
                for j2 in range(2):
                    bk, bb = next_bank()
                    for kc in range(nk):
                        mm(bk[:, :], wv_[:, kc, j2 * 128:(j2 + 1) * 128], rhs_l[kc], kc == 0, kc == nk - 1,
                           [Bw_] + list(rB), [bb])
                    if nm in ("ga", "gc"):
                        sg_, Bsg_ = tmp()
                        act(sg_[:, 0:512], bk[:, :], AF.Sigmoid, [bb], [Bsg_])
                        st_[(nm, j2)] = (sg_, Bsg_)
                    elif nm == "A":
                        sa, Bsa = st_[("ga", j2)]
                        vtt(sa[:, 0:512], bk[:, :], sa[:, 0:512], ALU.mult, [bb, Bsa], [Bsa])
                    else:
                        j = 2 * jp + j2
                        sa, Bsa = st_[("ga", j2)]
                        sc_, Bsc = st_[("gc", j2)]
                        vtt(sc_[:, 0:512], bk[:, :], sc_[:, 0:512], ALU.mult, [bb, Bsc], [Bsc])
                        vtt(mrgT[:, j, :], sa[:, 0:512], sc_[:, 0:512], ALU.add, [Bsa, Bsc], [B_mrg])
        while pending_ln2:
            pending_ln2.pop(0)()
        sch.dma("pool", S_lnp, lnp[:, :, :], ln_d[0:2].rearrange("a p d -> p a d"), writes=[B_lnp])
        for tb in range(4):
            sch.dma("pool", S_x1[tb], x1buf[:, tb, :], x_d[t0 + tb * 128: t0 + (tb + 1) * 128, :], writes=[B_x1[tb]])
        for cb in range(4):
            ws = [load_slab_bf(wb_o[k8 * 1024:(k8 + 1) * 1024, cb * 512:(cb + 1) * 512].rearrange(
                "(kc p) c -> p kc c", p=128), 8, 512, B_grp["g3"]) for k8 in range(2)]
            for tb in range(4):
                bk, bb = next_bank()
                for kc in range(16):
                    wv_, Bw_ = ws[kc // 8]
                    mm(bk[:, :], mrgT[:, kc, tb * 128:(tb + 1) * 128], wv_[:, kc % 8, :],
                       kc == 0, kc == 15, [Bw_, B_mrg], [bb])
                xv = x1buf[:, tb, cb * 512:(cb + 1) * 512]
                vstt(xv, xv, ALPHA, bk[:, :], ALU.mult, ALU.add, [bb, B_x1[tb]], [B_x1[tb]])
                if cb == 3:
                    ln_stats(tb)
        banks8 = [next_bank() for _ in range(8)]
        pending_ln1 = []
        for tb in range(4):
            sl = tb % 2
            sm = small[:, tb, :]
            act(xin[:, sl, :], x1buf[:, tb, :], AF.Identity, [B_x1[tb], B_small[tb]], [B_xin[sl]],
                bias=sm[:, 27:28], scale=sm[:, 26:27])
            for kc in range(16):
                bk, bb = banks8[kc // 2]
                bkb = bk[:, :].bitcast(BF16)
                off = (kc % 2) * 512 + tb * 128
                tp(bkb[:, off:off + 128], xin[:, sl, kc * 128:(kc + 1) * 128], [B_xin[sl], B_const], [bb])

            def fin(tb=tb):
                ln_finish(tb)
                if debug:
                    sch.dma("pool", S_dbg, dbg_x1[t0 + tb * 128:t0 + (tb + 1) * 128, :], x1buf[:, tb, :],
                            reads=[B_x1[tb]])
            pending_ln1.append(fin)
        for kc in range(16):
            bk, bb = banks8[kc // 2]
            bkb = bk[:, :].bitcast(BF16)
            src_ = bkb[:, (kc % 2) * 512:(kc % 2 + 1) * 512]
            if (kc // 2) % 2 == 0:
                act(x1T[:, kc, :], src_, AF.Identity, [bb, B_const], [B_x1T[kc]],
                    scale=lnT[:, kc:kc + 1], bias=lnT[:, 16 + kc:17 + kc])
            else:
                vts(x1T[:, kc, :], src_, lnT[:, kc:kc + 1], lnT[:, 16 + kc:17 + kc], ALU.mult, ALU.add,
                    [bb, B_const], [B_x1T[kc]])
        B_hT.alias_from([B_xTt, B_cin, B_mrg])
        x1T_kc = [x1T[:, kc, :] for kc in range(16)]
        for jp in range(NFC // 2):
            wgg, Bwgg = load_slab(colslab(w_g, jp * 256, 256), 16, 256)
            wuu, Bwuu = load_slab(colslab(w_u, jp * 256, 256), 16, 256)
            sgs = {}
            for nm, wv_, Bw_ in (("g", wgg, Bwgg), ("u", wuu, Bwuu)):
                for j2 in range(2):
                    bk, bb = next_bank()
                    for kc in range(16):
                        mm(bk[:, :], wv_[:, kc, j2 * 128:(j2 + 1) * 128], x1T_kc[kc], kc == 0, kc == 15,
                           [Bw_, B_x1T[kc]], [bb])
                    if nm == "g":
                        sg, Bsg = tmp()
                        act(sg[:, 0:512], bk[:, :], AF.Silu, [bb], [Bsg])
                        sgs[j2] = (sg, Bsg)
                    else:
                        sg, Bsg = sgs[j2]
                        vtt(hT[:, 2 * jp + j2, :], bk[:, :], sg[:, 0:512], ALU.mult, [bb, Bsg], [B_hT])
            if jp >= 1 and pending_ln1:
                pending_ln1.pop(0)()
        while pending_ln1:
            pending_ln1.pop(0)()
        sch.dma("pool", S_lnp, lnp[:, :, :], ln_d[2:4].rearrange("a p d -> p a d"), writes=[B_lnp])
        for cb in range(4):
            accs = [next_bank() for _ in range(4)]
            for k8 in range((NFC + 7) // 8):
                nkk = min(8, NFC - k8 * 8)
                wv_, Bw_ = load_slab(w_d[k8 * 1024:k8 * 1024 + nkk * 128, cb * 512:(cb + 1) * 512].rearrange(
                    "(kc p) c -> p kc c", p=128), nkk, 512)
                for kk in range(nkk):
                    kc = k8 * 8 + kk
                    for tb in range(4):
                        bk, bb = accs[tb]
                        mm(bk[:, :], hT[:, kc, tb * 128:(tb + 1) * 128], wv_[:, kk, :],
                           kc == 0, kc == NFC - 1, [Bw_, B_hT], [bb])
            for tb in range(4):
                bk, bb = accs[tb]
                xv = x1buf[:, tb, cb * 512:(cb + 1) * 512]
                vstt(xv, xv, ALPHA, bk[:, :], ALU.mult, ALU.add, [bb, B_x1[tb]], [B_x1[tb]])
        for tb in range(4):
            def ln2(tb=tb, t0=t0, last=(t == n_tiles - 1)):
                ln_stats(tb)
                ln_finish(tb)
                sch.dma("pool", S_out[tb], y_d[t0 + tb * 128:t0 + (tb + 1) * 128, :], x1buf[:, tb, :],
                        reads=[B_x1[tb]])
            pending_ln2.append(ln2)
        for bb_ in (B_xTt, B_cin, B_mrg):
            bb_.alias_from([B_hT])

    while pending_ln2:
        pending_ln2.pop(0)()

    with nc.Block() as block:
        sch.emit(nc, block, esems, out_sems)
    es.close()
    return nc


_NC_CACHE = {}


def _prep_inputs(inputs):
    f = np.float32
    w_in = np.ascontiguousarray(inputs["w_in"][0], dtype=f)
    cwt = np.ascontiguousarray(
        np.asarray(inputs["conv_w"][0], dtype=f).T.reshape(16, 128, 3).transpose(1, 0, 2).reshape(128, 48))
    lnp = np.stack([np.broadcast_to(np.asarray(inputs[k][0], dtype=f)[None, :], (128, D))
                    for k in ("ln1_g", "ln1_b", "ln2_g", "ln2_b")], axis=0)
    shared = {
        "w_in": w_in,
        "conv_w": cwt,
        "w_attn_o": np.ascontiguousarray(inputs["w_attn_o"][0], dtype=f),
        "w_conv_o": np.ascontiguousarray(inputs["w_conv_o"][0], dtype=f),
        "w_out": np.ascontiguousarray(inputs["w_out"][0], dtype=f),
        "w_ffn_gate": np.ascontiguousarray(inputs["w_ffn_gate"][0], dtype=f),
        "w_ffn_up": np.ascontiguousarray(inputs["w_ffn_up"][0], dtype=f),
        "w_ffn_down": np.ascontiguousarray(inputs["w_ffn_down"][0], dtype=f),
        "ln_params": np.ascontiguousarray(lnp),
        "ln_t": np.ascontiguousarray(np.concatenate([
            np.asarray(inputs["ln1_g"][0], dtype=f).reshape(16, 128).T,
            np.asarray(inputs["ln1_b"][0], dtype=f).reshape(16, 128).T], axis=1)),
    }
    x = np.asarray(inputs["x"], dtype=f)
    return [dict(shared, x=np.ascontiguousarray(x[b])) for b in range(NCORES)]


def kernel(**inputs):
    if "nc" not in _NC_CACHE:
        _NC_CACHE["nc"] = build_program()
    nc = _NC_CACHE["nc"]
    in_maps = _prep_inputs(inputs)
    res = run_bass_kernel_spmd(nc, in_maps, core_ids=list(range(NCORES)))
    return np.stack([np.asarray(r["y"], dtype=np.float32) for r in res.results], axis=0)
```

```python
import numpy as np
from contextlib import ExitStack
import concourse.bass as bass
import concourse.mybir as mybir
from concourse.bass_utils import run_bass_kernel_spmd

F32 = mybir.dt.float32
BF16 = mybir.dt.bfloat16
AF = mybir.ActivationFunctionType
ALU = mybir.AluOpType

D = 2048
S = 2048
NCORES = 8
HD = 128
NHG = 8
GROUPS = ((128, 1), (512, 4), (2048, 16))
AW = 3 * NHG * HD
FF = 5632
NFC = FF // 128
IN_COLS = 3 * AW + 3 * D + 2 * D
ALPHA = 2.0 ** 0.25
EPS = 1e-5
SCALE = HD ** -0.5
TT = 512
NT = S // TT

C_Q, C_K, C_V = 0, AW, 2 * AW
C_U, C_C, C_B = 3 * AW, 3 * AW + D, 3 * AW + 2 * D
C_GA, C_GC = 3 * AW + 3 * D, 3 * AW + 4 * D


class Buf:
    __slots__ = ("name", "writers", "readers")

    def __init__(self, name):
        self.name = name
        self.writers = {}
        self.readers = {}

    def alias_from(self, others):
        for o in others:
            for k, v in o.writers.items():
                self.readers[("w", k)] = v
            for k, v in o.readers.items():
                self.readers[("r", k)] = v


class Op:
    __slots__ = ("eng", "fn", "deps", "signal", "idx", "dma_sem", "dma_val", "key")

    def __init__(self, eng, fn):
        self.eng = eng
        self.fn = fn
        self.deps = []
        self.signal = False
        self.idx = None
        self.dma_sem = None
        self.dma_val = None
        self.key = eng


class DmaSem:
    __slots__ = ("sem", "count", "uid")
    _n = 0

    def __init__(self, sem):
        self.sem = sem
        self.count = 0
        DmaSem._n += 1
        self.uid = DmaSem._n


class Sched:
    ENGS = ("pe", "act", "dve", "pool", "sp")

    def __init__(self):
        self.streams = {e: [] for e in self.ENGS}

    def _link(self, op, reads, writes):
        deps = {}
        for b in reads:
            for w in b.writers.values():
                deps[id(w)] = (w, True)
        for b in writes:
            for r in b.readers.values():
                if id(r) not in deps:
                    deps[id(r)] = (r, False)
            for w in b.writers.values():
                if id(w) not in deps:
                    deps[id(w)] = (w, False)
        for d, raw in deps.values():
            if d is op:
                continue
            if d.dma_sem is None and op.dma_sem is None and d.eng == op.eng:
                if op.eng == "pe" or not raw:
                    continue
            if d.dma_sem is None:
                d.signal = True
            op.deps.append(d)
        for b in reads:
            b.readers[op.key] = op
        for b in writes:
            b.writers[op.key] = op

    def op(self, eng, fn, reads=(), writes=()):
        o = Op(eng, fn)
        self._link(o, reads, writes)
        self.streams[eng].append(o)
        return o

    def dma(self, queue, dsem, out, in_, reads=(), writes=()):
        o = Op(queue, lambda e: e.dma_start(out=out, in_=in_))
        dsem.count += 1
        o.dma_sem = dsem
        o.dma_val = 16 * dsem.count
        o.key = ("dma", dsem.uid)
        self._link(o, reads, writes)
        self.streams[queue].append(o)
        return o

    def emit(self, nc, block, esems, final_waits):
        for e in self.ENGS:
            n = 0
            for o in self.streams[e]:
                if o.dma_sem is None and o.signal:
                    n += 1
                    o.idx = n

        def run(ename, eng):
            waited = {}
            for o in self.streams[ename]:
                for d in o.deps:
                    if d.dma_sem is not None:
                        sem, val, k = d.dma_sem.sem, d.dma_val, ("d", d.dma_sem.uid)
                    else:
                        sem, val, k = esems[d.eng], d.idx, ("e", d.eng)
                    if waited.get(k, 0) >= val:
                        continue
                    waited[k] = val
                    eng.wait_ge(sem, val)
                ins = o.fn(eng)
                if o.dma_sem is not None:
                    ins.then_inc(o.dma_sem.sem, 16)
                elif o.signal:
                    ins.then_inc(esems[ename], 1)
            if ename == "sp":
                for ds in final_waits:
                    eng.wait_ge(ds.sem, 16 * ds.count)

        @block.tensor
        def _(t):
            run("pe", t)

        @block.scalar
        def _(s):
            run("act", s)

        @block.vector
        def _(v):
            run("dve", v)

        @block.gpsimd
        def _(g):
            run("pool", g)

        @block.sync
        def _(sp):
            run("sp", sp)


def build_program(debug=False, n_heads=NHG, n_tiles=NT):
    nc = bass.Bass("TRN2", target_bir_lowering=False)
    x_d = nc.dram_tensor("x", [S, D], F32, kind="ExternalInput").ap()
    w_in = nc.dram_tensor("w_in", [D, IN_COLS], F32, kind="ExternalInput").ap()
    conv_w = nc.dram_tensor("conv_w", [128, 16 * 3], F32, kind="ExternalInput").ap()
    w_ao = nc.dram_tensor("w_attn_o", [NHG * HD, D], F32, kind="ExternalInput").ap()
    w_co = nc.dram_tensor("w_conv_o", [D, D], F32, kind="ExternalInput").ap()
    w_o = nc.dram_tensor("w_out", [D, D], F32, kind="ExternalInput").ap()
    w_g = nc.dram_tensor("w_ffn_gate", [D, FF], F32, kind="ExternalInput").ap()
    w_u = nc.dram_tensor("w_ffn_up", [D, FF], F32, kind="ExternalInput").ap()
    w_d = nc.dram_tensor("w_ffn_down", [FF, D], F32, kind="ExternalInput").ap()
    ln_d = nc.dram_tensor("ln_params", [4, 128, D], F32, kind="ExternalInput").ap()
    lnT_d = nc.dram_tensor("ln_t", [128, 32], F32, kind="ExternalInput").ap()
    y_d = nc.dram_tensor("y", [S, D], F32, kind="ExternalOutput").ap()
    NB_IN = IN_COLS - C_U
    wb_in = nc.dram_tensor("wb_in", [D, NB_IN], BF16, kind="Internal").ap()
    wb_ao = nc.dram_tensor("wb_ao", [NHG * HD, D], BF16, kind="Internal").ap()
    wb_co = nc.dram_tensor("wb_co", [D, D], BF16, kind="Internal").ap()
    wb_o = nc.dram_tensor("wb_o", [D, D], BF16, kind="Internal").ap()
    wb_g = nc.dram_tensor("wb_g", [D, FF], BF16, kind="Internal").ap()
    wb_u = nc.dram_tensor("wb_u", [D, FF], BF16, kind="Internal").ap()
    wb_d = nc.dram_tensor("wb_d", [FF, D], BF16, kind="Internal").ap()
    if debug:
        dbg_attn = nc.dram_tensor("dbg_attn", [128, NHG, S], BF16, kind="ExternalOutput").ap()
        dbg_x1 = nc.dram_tensor("dbg_x1", [S, D], F32, kind="ExternalOutput").ap()

    es = ExitStack()
    sch = Sched()

    def sb(name, shape, dt):
        return es.enter_context(nc.sbuf_tensor(name, shape, dt))

    def mksem(name):
        return es.enter_context(nc.semaphore(name))

    def dsem(name):
        return DmaSem(mksem(name))

    ident = sb("ident", [128, 128], BF16)
    ones = sb("ones", [128, 128], BF16)
    maskLU = sb("maskLU", [128, 512], BF16)
    maskU4 = sb("maskU4", [128, 512], BF16)
    cw = sb("cw", [128, 48], F32)
    halo = sb("halo", [128, 32], F32)
    attnT = sb("attnT", [128, NHG, S], BF16)
    arena = sb("arena", [128, 16 * S], BF16)
    NSLAB = 4
    slabs = sb("slabs", [128, NSLAB, 4096], BF16)
    xin = sb("xin", [128, 2, D], BF16)
    x1buf = sb("x1buf", [128, 4, D], F32)
    lnp = sb("lnp", [128, 2, D], F32)
    NTMP = 10
    tmps = sb("tmps", [128, NTMP, 516], F32)
    small = sb("small", [128, 4, 32], F32)
    lnT = sb("lnT", [128, 32], F32)

    xT = arena[:, :].rearrange("p (c t) -> p c t", c=16)
    x1bf = x1buf[:, :, :].rearrange("p a d -> p (a d)").bitcast(BF16)
    qkT = x1bf[:, 0:8192].rearrange("p (s w t) -> p s w t", s=2, w=2)
    vsb = x1bf[:, 8192:12288].rearrange("p (s b d) -> p s b d", s=2, b=16)
    EPT = x1bf[:, 12288:14336].rearrange("p (s w c) -> p s w c", s=2, w=2)
    numden = lnp
    xTt = arena[:, 0:8192].rearrange("p (c t) -> p c t", c=16)
    cinT = arena[:, 8192:16384].rearrange("p (c t) -> p c t", c=16)
    mrgT = arena[:, 16384:24576].rearrange("p (c t) -> p c t", c=16)
    hT = arena[:, 0:NFC * TT].rearrange("p (c t) -> p c t", c=NFC)
    x1T = arena[:, 24576:32768].rearrange("p (c t) -> p c t", c=16)

    banks = [es.enter_context(nc.psum_tensor(f"pb{i}", [128, 512], F32)) for i in range(8)]
    bank_bufs = [Buf(f"bank{i}") for i in range(8)]
    bank_rr = [0]

    pool_rr = {}

    def next_bank(pool=None):
        if pool is None:
            i = bank_rr[0] % 8
            bank_rr[0] += 1
        else:
            k = pool_rr.get(pool, 0)
            pool_rr[pool] = k + 1
            i = pool[k % len(pool)]
        return banks[i], bank_bufs[i]

    P_ACC = (0, 1, 2, 3)
    P_S = (4, 5)
    P_PROJ = (6, 7)

    esems = {e: mksem(f"sem_{e}") for e in Sched.ENGS}

    B_const = Buf("const")
    B_xT = [Buf(f"xT{i}") for i in range(4)]
    B_attnT = [Buf(f"attnT{h}") for h in range(NHG)]
    B_xin = [Buf("xin0"), Buf("xin1")]
    S_xin = [dsem("s_xin0"), dsem("s_xin1")]
    B_slab = [Buf(f"slab{i}") for i in range(NSLAB)]
    S_slab = [dsem(f"s_slab{i}") for i in range(NSLAB)]
    S_slab_hw = [dsem(f"s_slabh{i}") for i in range(NSLAB)]
    slab_rr = [0]
    B_qk = [[Buf(f"qT{s}"), Buf(f"kT{s}")] for s in range(2)]
    B_v = [Buf("v0"), Buf("v1")]
    B_E = [Buf("E0"), Buf("E1")]
    B_PT = [Buf("PT0"), Buf("PT1")]
    B_num = Buf("num")
    B_den = Buf("den")
    B_small = [Buf(f"small{i}") for i in range(4)]

    def load_slab(src, a, b):
        i = slab_rr[0] % NSLAB
        slab_rr[0] += 1
        assert a * b <= 4096
        view = slabs[:, i, 0:a * b].rearrange("p (a b) -> p a b", a=a)
        sch.dma("pool", S_slab[i], view, src, writes=[B_slab[i]])
        return view, B_slab[i]

    def load_slab_bf(src, a, b, B_src):
        i = slab_rr[0] % NSLAB
        slab_rr[0] += 1
        assert a * b <= 4096
        view = slabs[:, i, 0:a * b].rearrange("p (a b) -> p a b", a=a)
        sch.dma("sp", S_slab_hw[i], view, src, reads=[B_src], writes=[B_slab[i]])
        return view, B_slab[i]

    G_names = ("g1", "g2", "g3", "g4", "g5")
    B_grp = {n: Buf("cv_" + n) for n in G_names}
    S_grp = {n: dsem("s_cv_" + n) for n in G_names}
    conv_list = []

    def add_conv(grp, dst, src_, rows, ncols):
        for r0 in range(0, rows, 128):
            conv_list.append((128 * ncols * 4, grp, dst[r0:r0 + 128, :], src_[r0:r0 + 128, :]))

    add_conv("g1", wb_in[:, 0:3 * D], w_in[:, C_U:C_U + 3 * D], D, 3 * D)
    add_conv("g2", wb_in[:, 3 * D:5 * D], w_in[:, C_GA:C_GA + 2 * D], D, 2 * D)
    add_conv("g2", wb_ao, w_ao, NHG * HD, D)
    add_conv("g2", wb_co, w_co, D, D)
    add_conv("g3", wb_o, w_o, D, D)
    conv_total = sum(c[0] for c in conv_list)
    conv_state = {"i": 0, "bytes": 0}

    def issue_conversions(frac):
        while conv_state["i"] < len(conv_list) and conv_state["bytes"] < frac * conv_total:
            nb, grp, dst, src_ = conv_list[conv_state["i"]]
            sch.dma("pool", S_grp[grp], dst, src_, writes=[B_grp[grp]])
            conv_state["i"] += 1
            conv_state["bytes"] += nb

    def colslab(w, c0, nc_=128):
        return w[:, c0:c0 + nc_].rearrange("(kc p) c -> p kc c", p=128)

    evac_rr = [0]

    def evac_engine():
        evac_rr[0] += 1
        return "act" if evac_rr[0] % 2 else "dve"

    def copy_op(eng, out, in_, reads, writes):
        if eng == "act":
            sch.op("act", lambda e: e.activation(out=out, in_=in_, func=AF.Copy), reads, writes)
        else:
            sch.op(eng, lambda e: e.tensor_copy(out=out, in_=in_), reads, writes)


    def mm(out, lhsT, rhs, start, stop, reads, writes):
        sch.op("pe", lambda e: e.matmul(out, lhsT, rhs, start=start, stop=stop), reads, writes)

    def tp(out, in_, reads, writes):
        sch.op("pe", lambda e: e.transpose(out=out, in_=in_, identity=ident[:, :]), reads, writes)

    def act(out, in_, func, reads, writes, **kw):
        sch.op("act", lambda e: e.activation(out=out, in_=in_, func=func, **kw), reads, writes)

    def vtt(out, in0, in1, op, reads, writes, eng="dve"):
        sch.op(eng, lambda e: e.tensor_tensor(out=out, in0=in0, in1=in1, op=op), reads, writes)

    def vts(out, in0, s1, s2, op0, op1, reads, writes):
        if s2 is None:
            sch.op("dve", lambda e: e.tensor_scalar(out=out, in0=in0, scalar1=s1, scalar2=None, op0=op0), reads, writes)
        else:
            sch.op("dve", lambda e: e.tensor_scalar(out=out, in0=in0, scalar1=s1, scalar2=s2, op0=op0, op1=op1), reads, writes)

    def vstt(out, in0, scalar, in1, op0, op1, reads, writes):
        sch.op("dve", lambda e: e.scalar_tensor_tensor(out=out, in0=in0, scalar=scalar, in1=in1, op0=op0, op1=op1),
               reads, writes)

    def vcp(out, in_, reads, writes, eng="dve"):
        sch.op(eng, lambda e: e.tensor_copy(out=out, in_=in_), reads, writes)

    sch.op("pool", lambda e: e.memset(ident[:, :], 1.0), writes=[B_const])
    sch.op("pool", lambda e: e.affine_select(out=ident[:, :], in_=ident[:, :], pattern=[[-1, 128]],
                                             compare_op=ALU.is_equal, fill=0.0, base=0,
                                             channel_multiplier=1), reads=[B_const], writes=[B_const])
    sch.op("pool", lambda e: e.memset(ones[:, :], 1.0), writes=[B_const])
    sch.op("pool", lambda e: e.memset(maskLU[:, :], 1.0), writes=[B_const])
    sch.op("pool", lambda e: e.memset(maskU4[:, :], 1.0), writes=[B_const])
    for s in range(4):
        cs = slice(s * 128, (s + 1) * 128)
        if s % 2 == 0:
            sch.op("pool", lambda e, cs=cs: e.affine_select(
                out=maskLU[:, cs], in_=maskLU[:, cs], pattern=[[-1, 128]], compare_op=ALU.is_ge,
                fill=0.0, base=0, channel_multiplier=1), reads=[B_const], writes=[B_const])
        else:
            sch.op("pool", lambda e, cs=cs: e.affine_select(
                out=maskLU[:, cs], in_=maskLU[:, cs], pattern=[[1, 128]], compare_op=ALU.is_ge,
                fill=0.0, base=0, channel_multiplier=-1), reads=[B_const], writes=[B_const])
        sch.op("pool", lambda e, cs=cs: e.affine_select(
            out=maskU4[:, cs], in_=maskU4[:, cs], pattern=[[1, 128]], compare_op=ALU.is_ge,
            fill=0.0, base=0, channel_multiplier=-1), reads=[B_const], writes=[B_const])
    S_cw = dsem("s_cw")
    sch.dma("sp", S_cw, cw[:, :], conv_w, writes=[B_const])
    sch.dma("sp", S_cw, lnT[:, :], lnT_d, writes=[B_const])

    B_halo = Buf("halo")
    sch.op("pool", lambda e: e.memset(halo[:, :], 0.0), writes=[B_halo])

    def load_xT(dst, B_dst, tok0, nblk, after_tb=None):
        for tb in range(nblk):
            Bd = B_dst[tb // 4] if isinstance(B_dst, list) else B_dst
            sl = tb % 2
            sch.dma("pool", S_xin[sl], xin[:, sl, :], x_d[tok0 + tb * 128: tok0 + (tb + 1) * 128, :],
                    writes=[B_xin[sl]])
            for half in range(2):
                bk, bb = next_bank()
                bkb = bk[:, :].bitcast(BF16)
                for c in range(8):
                    kc = half * 8 + c
                    tp(bkb[:, c * 128:(c + 1) * 128], xin[:, sl, kc * 128:(kc + 1) * 128],
                       [B_xin[sl], B_const], [bb])
                copy_op(evac_engine(), dst[:, half * 8:(half + 1) * 8, tb * 128:(tb + 1) * 128],
                        bkb.rearrange("p (c t) -> p c t", c=8), [bb], [Bd])
            if after_tb is not None:
                after_tb(tb)

    def tok_ap(T2d, dil, r, n):
        if dil == 1:
            return T2d[:, n * 128:(n + 1) * 128]
        return T2d.rearrange("p (l r) -> p r l", r=dil)[:, r, n * 128:(n + 1) * 128]

    xT_tiles = [[xT[:, kc, tt * 512:(tt + 1) * 512] for kc in range(16)] for tt in range(4)]

    def make_head(h, g, dil, sl):
        n_blk = 16 // dil
        col = g * NHG * HD + h * HD
        qT = qkT[:, sl, 0, :]
        kT = qkT[:, sl, 1, :]
        BqT, BkT = B_qk[sl]
        W = {}

        def load_w():
            W["q"] = load_slab(colslab(w_in, C_Q + col), 16, 128)
            W["k"] = load_slab(colslab(w_in, C_K + col), 16, 128)
            W["v"] = load_slab(colslab(w_in, C_V + col), 16, 128)
            conv_state["heads"] = conv_state.get("heads", 0) + 1
            if conv_state["heads"] >= 2:
                issue_conversions((conv_state["heads"] - 1) / (2.25 * n_heads))

        def proj_group(which, dst, B_dst, tt):
            if not W:
                load_w()
            wv, B_w = W[which]
            bk, bb = next_bank(P_PROJ)
            for kc in range(16):
                mm(bk[:, :], wv[:, kc, :], xT_tiles[tt][kc], kc == 0, kc == 15, [B_w, B_xT[tt]], [bb])
            copy_op(evac_engine(), dst[:, tt * 512:(tt + 1) * 512], bk[:, :], [bb], [B_dst])

        def v_group(b4):
            if not W:
                load_w()
            wvv, Bwv = W["v"]
            bk, bb = next_bank(P_PROJ)
            for j in range(4):
                b = b4 * 4 + j
                r, n = b // n_blk, b % n_blk
                for kc in range(16):
                    mm(bk[:, j * 128:(j + 1) * 128], tok_ap(xT[:, kc, :], dil, r, n), wvv[:, kc, :],
                       kc == 0, kc == 15, ([B_xT[b4]] if dil == 1 else B_xT) + [Bwv], [bb])
            copy_op(evac_engine(), vsb[:, sl, b4 * 4:(b4 + 1) * 4, :],
                    bk[:, :].rearrange("p (b d) -> p b d", b=4), [bb], [B_v[sl]])

        psteps = []
        for tt in range(4):
            psteps.append(lambda tt=tt: proj_group("q", qT, BqT, tt))
        for tt in range(4):
            psteps.append(lambda tt=tt: proj_group("k", kT, BkT, tt))
        for b4 in range(4):
            psteps.append(lambda b4=b4: v_group(b4))

        units = []
        if dil == 16:
            for u in range(4):
                units.append([(b, b) for b in range(4 * u, 4 * u + 4)])
            mask = maskU4
        else:
            for u in range(8):
                ent = []
                for b in (2 * u, 2 * u + 1):
                    n = b % n_blk
                    ent.append((b - 1 if n > 0 else None, b))
                    ent.append((b, b))
                units.append(ent)
            mask = maskLU
        nu = len(units)
        ust = {}

        def issue_S(u):
            bk, bb = next_bank(P_S)
            ent = units[u]
            c0 = 128 if ent[0][0] is None else 0
            for s_i, (kb, qb) in enumerate(ent):
                if kb is None:
                    continue
                mm(bk[:, s_i * 128:(s_i + 1) * 128], tok_ap(kT, dil, kb // n_blk, kb % n_blk),
                   tok_ap(qT, dil, qb // n_blk, qb % n_blk), True, True, [BqT, BkT], [bb])
            e_ = u % 2
            Ev = EPT[:, e_, 0, :]
            Pv = EPT[:, e_, 1, :]
            act(Ev[:, c0:512], bk[:, c0:512], AF.Exp, [bb], [B_E[e_]], scale=SCALE)
            vtt(Pv[:, c0:512], Ev[:, c0:512], mask[:, c0:512], ALU.mult, [B_E[e_], B_const], [B_PT[e_]])
            ust[u] = (Pv, B_PT[e_])

        acc = {}

        def issue_PV(u):
            Pv, BP = ust[u]
            ent = units[u]
            qbs = sorted(set(q for _, q in ent))
            for qb in qbs:
                if qb % 4 == 0:
                    acc["o"] = next_bank(P_ACC)
                    acc["d"] = next_bank(P_ACC)
                sl_e = [(s_i, kb) for s_i, (kb, q) in enumerate(ent) if q == qb and kb is not None]
                qs = qb % 4
                for which in ("o", "d"):
                    tbk, tbb = acc[which]
                    for i_, (s_i, kb) in enumerate(sl_e):
                        lhs = vsb[:, sl, kb, :] if which == "o" else ones[:, :]
                        rd = [B_v[sl], BP] if which == "o" else [B_const, BP]
                        mm(tbk[:, qs * 128:(qs + 1) * 128], lhs, Pv[:, s_i * 128:(s_i + 1) * 128],
                           i_ == 0, i_ == len(sl_e) - 1, rd, [tbb])
                if qb % 4 == 3:
                    j = qb // 4
                    for which, Bacc in (("o", B_num), ("d", B_den)):
                        tbk, tbb = acc[which]
                        A = numden[:, 0 if which == "o" else 1, :]
                        if dil == 1:
                            dst = A[:, j * 512:(j + 1) * 512]
                            src_ = tbk[:, :]
                        elif dil == 4:
                            dst = A.rearrange("p (l r) -> p r l", r=4)[:, j, :]
                            src_ = tbk[:, :]
                        else:
                            dst = A.rearrange("p (i r) -> p r i", r=16)[:, 4 * j:4 * j + 4, :]
                            src_ = tbk[:, :].rearrange("p (r i) -> p r i", r=4)
                        if g == 0:
                            vcp(dst, src_, [tbb], [Bacc])
                        else:
                            vtt(dst, src_, dst, ALU.add, [tbb, Bacc], [Bacc])

        LOOK = 2
        asteps = []
        for u in range(min(LOOK, nu)):
            asteps.append(lambda u=u: issue_S(u))
        for u in range(nu):
            def st(u=u):
                issue_PV(u)
                if u + LOOK < nu:
                    issue_S(u + LOOK)
            asteps.append(st)
        fins = []
        if g == 2:
            for qd in range(1):
                def fin(qd=qd):
                    cs = slice(0, 2048)
                    sch.op("dve", (lambda o_: (lambda e: e.reciprocal(out=o_, in_=o_)))(numden[:, 1, cs]),
                           [B_den], [B_den])
                    vtt(attnT[:, h, cs], numden[:, 0, cs], numden[:, 1, cs], ALU.mult, [B_num, B_den], [B_attnT[h]])
                fins.append(fin)
        return psteps, asteps, fins

    heads = []
    for h in range(n_heads):
        for g, (window, dil) in enumerate(GROUPS):
            heads.append(make_head(h, g, dil, len(heads) % 2))
    def after_tb(tb):
        if tb % 4 == 3:
            tt = tb // 4
            for idx in (tt, 4 + tt, 8 + tt):
                heads[0][0][idx]()
    load_xT(xT, B_xT, 0, 16, after_tb)
    prev_fins = []
    for i, (_, A_, F_) in enumerate(heads):
        P_ = heads[i + 1][0] if i + 1 < len(heads) else []
        k = 0
        for si, a_ in enumerate(A_):
            a_()
            while prev_fins:
                prev_fins.pop(0)()
            target = ((si + 1) * len(P_)) // len(A_)
            while k < target:
                P_[k]()
                k += 1
        assert not prev_fins
        prev_fins = list(F_)
    for f_ in prev_fins:
        f_()

    out_sems = []
    if debug:
        S_dbg = dsem("s_dbg")
        sch.dma("pool", S_dbg, dbg_attn, attnT[:, :, :], reads=B_attnT)
        out_sems.append(S_dbg)

    issue_conversions(2.0)
    B_lnp = Buf("lnp")
    B_lnp.alias_from([B_num, B_den])
    S_lnp = dsem("s_lnp")
    B_x1 = [Buf(f"x1_{i}") for i in range(4)]
    for bx in B_x1:
        bx.alias_from(B_qk[0] + B_qk[1] + B_v + B_E + B_PT)
    S_x1 = [dsem(f"s_x1_{i}") for i in range(4)]
    S_out = [dsem(f"s_out_{i}") for i in range(4)]
    out_sems += S_out
    B_xTt = Buf("xTt")
    B_cin = Buf("cinT")
    B_mrg = Buf("mrgT")
    B_hT = Buf("hT")
    B_x1T = [Buf(f"x1T{i}") for i in range(16)]
    for bb_ in [B_xTt, B_cin, B_mrg] + B_x1T:
        bb_.alias_from(B_xT)
    B_tmp = [Buf(f"tmp{i}") for i in range(NTMP)]
    tmp_rr = [0]

    def tmp():
        i = tmp_rr[0] % NTMP
        tmp_rr[0] += 1
        return tmps[:, i, :], B_tmp[i]

    def cws(j, tap):
        return cw[:, j * 3 + tap: j * 3 + tap + 1]

    def ln_stats(tb):
        yv = x1buf[:, tb, :]
        Bx, Bs = B_x1[tb], B_small[tb]
        sm = small[:, tb, :]
        for c in range(4):
            sch.op("dve", (lambda o_, i_: (lambda e: e.bn_stats(out=o_, in_=i_)))(
                sm[:, c * 6:(c + 1) * 6], yv[:, c * 512:(c + 1) * 512]), [Bx], [Bs])
        sch.op("dve", lambda e: e.bn_aggr(out=sm[:, 24:26], in_=sm[:, 0:24]), [Bs], [Bs])
        act(sm[:, 28:29], sm[:, 25:26], AF.Sqrt, [Bs], [Bs], bias=EPS, scale=1.0)
        sch.op("dve", lambda e: e.reciprocal(out=sm[:, 26:27], in_=sm[:, 28:29]), [Bs], [Bs])
        vstt(sm[:, 27:28], sm[:, 24:25], -1.0, sm[:, 26:27], ALU.mult, ALU.mult, [Bs], [Bs])

    def ln_finish(tb, bias_eng="dve"):
        yv = x1buf[:, tb, :]
        Bx, Bs = B_x1[tb], B_small[tb]
        sm = small[:, tb, :]
        act(yv, yv, AF.Identity, [Bx, Bs], [Bx], bias=sm[:, 27:28], scale=sm[:, 26:27])
        vtt(yv, yv, lnp[:, 0, :], ALU.mult, [Bx, B_lnp], [Bx])
        vtt(yv, yv, lnp[:, 1, :], ALU.add, [Bx, B_lnp], [Bx], eng=bias_eng)

    pending_ln2 = []

    for t in range(n_tiles):
        t0 = t * TT
        load_xT(xTt, B_xTt, t0, 4)
        xTt_kc = [xTt[:, kc, :] for kc in range(16)]
        for jp in range(8):
            wu2, Bwu = load_slab_bf(colslab(wb_in, jp * 256, 256), 16, 256, B_grp['g1'])
            wc2, Bwc = load_slab_bf(colslab(wb_in, D + jp * 256, 256), 16, 256, B_grp['g1'])
            wb2, Bwb = load_slab_bf(colslab(wb_in, 2 * D + jp * 256, 256), 16, 256, B_grp['g1'])
            pbs = {}
            for nm, wv_, Bw_ in (("u", wu2, Bwu), ("c", wc2, Bwc), ("b", wb2, Bwb)):
                for j2 in range(2):
                    bk, bb = next_bank()
                    for kc in range(16):
                        mm(bk[:, :], wv_[:, kc, j2 * 128:(j2 + 1) * 128], xTt_kc[kc], kc == 0, kc == 15,
                           [Bw_, B_xTt], [bb])
                    pbs[(nm, j2)] = (bk, bb)
                    if nm == "u":
                        us, Bus = tmp()
                        act(us[:, 0:512], bk[:, :], AF.Copy, [bb], [Bus])
                        pbs[("us", j2)] = (us, Bus)
                    elif nm == "c":
                        j = 2 * jp + j2
                        us, Bus = pbs[("us", j2)]
                        zs, Bzs = tmp()
                        vcp(zs[:, 0:2], halo[:, 2 * j:2 * j + 2], [B_halo], [Bzs])
                        vtt(zs[:, 2:514], bk[:, :], us[:, 0:512], ALU.mult, [bb, Bus], [Bzs])
                        vcp(halo[:, 2 * j:2 * j + 2], zs[:, 512:514], [Bzs], [B_halo])
                        ya, Bya = us, Bus
                        vts(ya[:, 0:512], zs[:, 2:514], cws(j, 0), None, ALU.mult, None, [Bzs, B_const], [Bya])
                        vstt(ya[:, 0:512], zs[:, 1:513], cws(j, 1), ya[:, 0:512], ALU.mult, ALU.add,
                             [Bzs, Bya, B_const], [Bya])
                        vstt(ya[:, 0:512], zs[:, 0:512], cws(j, 2), ya[:, 0:512], ALU.mult, ALU.add,
                             [Bzs, Bya, B_const], [Bya])
                    else:
                        j = 2 * jp + j2
                        ya, Bya = pbs[("us", j2)]
                        vtt(cinT[:, j, :], bk[:, :], ya[:, 0:512], ALU.mult, [bb, Bya], [B_cin])
            if pending_ln2:
                pending_ln2.pop(0)()
        attn_kc = [attnT[:, kc, t0:t0 + TT] for kc in range(NHG)]
        cin_kc = [cinT[:, kc, :] for kc in range(16)]
        for jp in range(8):
            wga, Bwga = load_slab_bf(colslab(wb_in, 3 * D + jp * 256, 256), 16, 256, B_grp['g2'])
            wao, Bwao = load_slab_bf(colslab(wb_ao, jp * 256, 256), 8, 256, B_grp['g2'])
            wgc, Bwgc = load_slab_bf(colslab(wb_in, 4 * D + jp * 256, 256), 16, 256, B_grp['g2'])
            wco, Bwco = load_slab_bf(colslab(wb_co, jp * 256, 256), 16, 256, B_grp['g2'])
            st_ = {}
            for nm, wv_, Bw_, rhs_l, rB in (("ga", wga, Bwga, xTt_kc, [B_xTt]), ("A", wao, Bwao, attn_kc, B_attnT),
                                            ("gc", wgc, Bwgc, xTt_kc, [B_xTt]), ("C", wco, Bwco, cin_kc, [B_cin])):
                nk = len(rhs_l)
                for j2 in range(2):
                    bk, bb = next_bank()
                    for kc in range(nk):
                        mm(bk[:, :], wv_[:, kc, j2 * 128:(j2 + 1) * 128], rhs_l[kc], kc == 0, kc == nk - 1,
                           [Bw_] + list(rB), [bb])
                    if nm in ("ga", "gc"):
                        sg_, Bsg_ = tmp()
                        act(sg_[:, 0:512], bk[:, :], AF.Sigmoid, [bb], [Bsg_])
                        st_[(nm, j2)] = (sg_, Bsg_)
                    elif nm == "A":
                        sa, Bsa = st_[("ga", j2)]
                        vtt(sa[:, 0:512], bk[:, :], sa[:, 0:512], ALU.mult, [bb, Bsa], [Bsa])
                    else:
                        j = 2 * jp + j2
                        sa, Bsa = st_[("ga", j2)]
                        sc_, Bsc = st_[("gc", j2)]
                        vtt(sc_[:, 0:512], bk[:, :], sc_[:, 0:512], ALU.mult, [bb, Bsc], [Bsc])
                        vtt(mrgT[:, j, :], sa[:, 0:512], sc_[:, 0:512], ALU.add, [Bsa, Bsc], [B_mrg])
        while pending_ln2:
            pending_ln2.pop(0)()
        sch.dma("pool", S_lnp, lnp[:, :, :], ln_d[0:2].rearrange("a p d -> p a d"), writes=[B_lnp])
        for tb in range(4):
            sch.dma("pool", S_x1[tb], x1buf[:, tb, :], x_d[t0 + tb * 128: t0 + (tb + 1) * 128, :], writes=[B_x1[tb]])
        for cb in range(4):
            ws = [load_slab_bf(wb_o[k8 * 1024:(k8 + 1) * 1024, cb * 512:(cb + 1) * 512].rearrange(
                "(kc p) c -> p kc c", p=128), 8, 512, B_grp["g3"]) for k8 in range(2)]
            for tb in range(4):
                bk, bb = next_bank()
                for kc in range(16):
                    wv_, Bw_ = ws[kc // 8]
                    mm(bk[:, :], mrgT[:, kc, tb * 128:(tb + 1) * 128], wv_[:, kc % 8, :],
                       kc == 0, kc == 15, [Bw_, B_mrg], [bb])
                xv = x1buf[:, tb, cb * 512:(cb + 1) * 512]
                vstt(xv, xv, ALPHA, bk[:, :], ALU.mult, ALU.add, [bb, B_x1[tb]], [B_x1[tb]])
                if cb == 3:
                    ln_stats(tb)
        banks8 = [next_bank() for _ in range(8)]
        pending_ln1 = []
        for tb in range(4):
            sl = tb % 2
            sm = small[:, tb, :]
            act(xin[:, sl, :], x1buf[:, tb, :], AF.Identity, [B_x1[tb], B_small[tb]], [B_xin[sl]],
                bias=sm[:, 27:28], scale=sm[:, 26:27])
            for kc in range(16):
                bk, bb = banks8[kc // 2]
                bkb = bk[:, :].bitcast(BF16)
                off = (kc % 2) * 512 + tb * 128
                tp(bkb[:, off:off + 128], xin[:, sl, kc * 128:(kc + 1) * 128], [B_xin[sl], B_const], [bb])

            def fin(tb=tb):
                ln_finish(tb)
                if debug:
                    sch.dma("pool", S_dbg, dbg_x1[t0 + tb * 128:t0 + (tb + 1) * 128, :], x1buf[:, tb, :],
                            reads=[B_x1[tb]])
            pending_ln1.append(fin)
        for kc in range(16):
            bk, bb = banks8[kc // 2]
            bkb = bk[:, :].bitcast(BF16)
            src_ = bkb[:, (kc % 2) * 512:(kc % 2 + 1) * 512]
            if (kc // 2) % 2 == 0:
                act(x1T[:, kc, :], src_, AF.Identity, [bb, B_const], [B_x1T[kc]],
                    scale=lnT[:, kc:kc + 1], bias=lnT[:, 16 + kc:17 + kc])
            else:
                vts(x1T[:, kc, :], src_, lnT[:, kc:kc + 1], lnT[:, 16 + kc:17 + kc], ALU.mult, ALU.add,
                    [bb, B_const], [B_x1T[kc]])
        B_hT.alias_from([B_xTt, B_cin, B_mrg])
        x1T_kc = [x1T[:, kc, :] for kc in range(16)]
        for jp in range(NFC // 2):
            wgg, Bwgg = load_slab(colslab(w_g, jp * 256, 256), 16, 256)
            wuu, Bwuu = load_slab(colslab(w_u, jp * 256, 256), 16, 256)
            sgs = {}
            for nm, wv_, Bw_ in (("g", wgg, Bwgg), ("u", wuu, Bwuu)):
                for j2 in range(2):
                    bk, bb = next_bank()
                    for kc in range(16):
                        mm(bk[:, :], wv_[:, kc, j2 * 128:(j2 + 1) * 128], x1T_kc[kc], kc == 0, kc == 15,
                           [Bw_, B_x1T[kc]], [bb])
                    if nm == "g":
                        sg, Bsg = tmp()
                        act(sg[:, 0:512], bk[:, :], AF.Silu, [bb], [Bsg])
                        sgs[j2] = (sg, Bsg)
                    else:
                        sg, Bsg = sgs[j2]
                        vtt(hT[:, 2 * jp + j2, :], bk[:, :], sg[:, 0:512], ALU.mult, [bb, Bsg], [B_hT])
            if jp >= 1 and pending_ln1:
                pending_ln1.pop(0)()
        while pending_ln1:
            pending_ln1.pop(0)()
        sch.dma("pool", S_lnp, lnp[:, :, :], ln_d[2:4].rearrange("a p d -> p a d"), writes=[B_lnp])
        for cb in range(4):
            accs = [next_bank() for _ in range(4)]
            for k8 in range((NFC + 7) // 8):
                nkk = min(8, NFC - k8 * 8)
                wv_, Bw_ = load_slab(w_d[k8 * 1024:k8 * 1024 + nkk * 128, cb * 512:(cb + 1) * 512].rearrange(
                    "(kc p) c -> p kc c", p=128), nkk, 512)
                for kk in range(nkk):
                    kc = k8 * 8 + kk
                    for tb in range(4):
                        bk, bb = accs[tb]
                        mm(bk[:, :], hT[:, kc, tb * 128:(tb + 1) * 128], wv_[:, kk, :],
                           kc == 0, kc == NFC - 1, [Bw_, B_hT], [bb])
            for tb in range(4):
                bk, bb = accs[tb]
                xv = x1buf[:, tb, cb * 512:(cb + 1) * 512]
                vstt(xv, xv, ALPHA, bk[:, :], ALU.mult, ALU.add, [bb, B_x1[tb]], [B_x1[tb]])
        for tb in range(4):
            def ln2(tb=tb, t0=t0, last=(t == n_tiles - 1)):
                ln_stats(tb)
                ln_finish(tb)
                sch.dma("pool", S_out[tb], y_d[t0 + tb * 128:t0 + (tb + 1) * 128, :], x1buf[:, tb, :],
                        reads=[B_x1[tb]])
            pending_ln2.append(ln2)
        for bb_ in (B_xTt, B_cin, B_mrg):
            bb_.alias_from([B_hT])

    while pending_ln2:
        pending_ln2.pop(0)()

    with nc.Block() as block:
        sch.emit(nc, block, esems, out_sems)
    es.close()
    return nc


_NC_CACHE = {}


def _prep_inputs(inputs):
    f = np.float32
    w_in = np.ascontiguousarray(inputs["w_in"][0], dtype=f)
    cwt = np.ascontiguousarray(
        np.asarray(inputs["conv_w"][0], dtype=f).T.reshape(16, 128, 3).transpose(1, 0, 2).reshape(128, 48))
    lnp = np.stack([np.broadcast_to(np.asarray(inputs[k][0], dtype=f)[None, :], (128, D))
                    for k in ("ln1_g", "ln1_b", "ln2_g", "ln2_b")], axis=0)
    shared = {
        "w_in": w_in,
        "conv_w": cwt,
        "w_attn_o": np.ascontiguousarray(inputs["w_attn_o"][0], dtype=f),
        "w_conv_o": np.ascontiguousarray(inputs["w_conv_o"][0], dtype=f),
        "w_out": np.ascontiguousarray(inputs["w_out"][0], dtype=f),
        "w_ffn_gate": np.ascontiguousarray(inputs["w_ffn_gate"][0], dtype=f),
        "w_ffn_up": np.ascontiguousarray(inputs["w_ffn_up"][0], dtype=f),
        "w_ffn_down": np.ascontiguousarray(inputs["w_ffn_down"][0], dtype=f),
        "ln_params": np.ascontiguousarray(lnp),
        "ln_t": np.ascontiguousarray(np.concatenate([
            np.asarray(inputs["ln1_g"][0], dtype=f).reshape(16, 128).T,
            np.asarray(inputs["ln1_b"][0], dtype=f).reshape(16, 128).T], axis=1)),
    }
    x = np.asarray(inputs["x"], dtype=f)
    return [dict(shared, x=np.ascontiguousarray(x[b])) for b in range(NCORES)]


def kernel(**inputs):
    if "nc" not in _NC_CACHE:
        _NC_CACHE["nc"] = build_program()
    nc = _NC_CACHE["nc"]
    in_maps = _prep_inputs(inputs)
    res = run_bass_kernel_spmd(nc, in_maps, core_ids=list(range(NCORES)))
    return np.stack([np.asarray(r["y"], dtype=np.float32) for r in res.results], axis=0)
```
